# Optimizing a Trainium2 kernel written in Bass

```python
import jax, jax.numpy as jnp
from jax import lax
import numpy as np

D_MODEL = 2048
BATCH = 2
SEQ = 4096
DEPTH = 2

HEAD_DIM = 128
A_Q_HEADS = 8
A_KV_HEADS = 2
B_GROUPS = ((128, 1), (512, 4), (2048, 16))
B_HEADS_PER_GROUP = 2
B_HEADS = B_HEADS_PER_GROUP * len(B_GROUPS)
B_Q_BLOCK = 64
C_Q_HEADS = 8
C_KV_HEADS = 2
C_HALF_WINDOW = 128
Q_BLOCK = 128
N_BRANCHES = 3
GRID_W = 64
ROPE_THETA = 10000.0
PEER_HEADS = 8
PEER_N_KEYS = 128
PEER_N_EXPERTS = PEER_N_KEYS * PEER_N_KEYS
PEER_QUERY_DIM = 256
PEER_TOPK = 16
PEER_TOKEN_CHUNK = 128
LN_EPS = 1e-5
QK_EPS = 1e-6
NEG_INF = -1e30
DEEPNORM_ALPHA = (2 * DEPTH) ** 0.25
DEEPNORM_BETA = (8 * DEPTH) ** -0.25

A_Q_W = A_Q_HEADS * HEAD_DIM
A_KV_W = A_KV_HEADS * HEAD_DIM
B_W = B_HEADS * HEAD_DIM
C_Q_W = C_Q_HEADS * HEAD_DIM
C_KV_W = C_KV_HEADS * HEAD_DIM
IN_SPLITS = (A_Q_W, A_KV_W, A_KV_W, B_W, B_W, B_W, C_Q_W, C_KV_W, C_KV_W, N_BRANCHES * D_MODEL)
IN_WIDTH = sum(IN_SPLITS)

kernel_name = "hybrid_gated_mixers_peer_deepnorm_encoder"


def layer_norm(x, g, b):
    xf = x.astype(jnp.float32)
    mu = jnp.mean(xf, axis=-1, keepdims=True)
    var = jnp.mean(jnp.square(xf - mu), axis=-1, keepdims=True)
    return ((xf - mu) * lax.rsqrt(var + LN_EPS) * g + b).astype(x.dtype)


def rms_norm(x, g):
    xf = x.astype(jnp.float32)
    return (xf * lax.rsqrt(jnp.mean(jnp.square(xf), axis=-1, keepdims=True) + QK_EPS) * g).astype(x.dtype)


def alibi_slopes(n):
    return jnp.asarray(2.0 ** (-8.0 * np.arange(1, n + 1) / n), dtype=jnp.float32)


def axial_rope_tables(seq):
    rows = seq // GRID_W
    row = jnp.repeat(jnp.arange(rows, dtype=jnp.float32), GRID_W)
    col = jnp.tile(jnp.arange(GRID_W, dtype=jnp.float32), rows)
    quarter = HEAD_DIM // 4
    inv = ROPE_THETA ** (-jnp.arange(quarter, dtype=jnp.float32) / quarter)
    ang_r = row[:, None] * inv
    ang_c = col[:, None] * inv
    return jnp.cos(ang_r), jnp.sin(ang_r), jnp.cos(ang_c), jnp.sin(ang_c)


def axial_rope(x, cos_r, sin_r, cos_c, sin_c):
    xf = x.astype(jnp.float32)
    bshape = (1, x.shape[1]) + (1,) * (x.ndim - 3) + (-1,)

    def rot(xh, cos, sin):
        cos = cos.reshape(bshape)
        sin = sin.reshape(bshape)
        x1, x2 = jnp.split(xh, 2, axis=-1)
        return jnp.concatenate([x1 * cos - x2 * sin, x2 * cos + x1 * sin], axis=-1)

    xr, xc = jnp.split(xf, 2, axis=-1)
    return jnp.concatenate([rot(xr, cos_r, sin_r), rot(xc, cos_c, sin_c)], axis=-1).astype(x.dtype)


def banded_attention(q, k, v, half_win, q_block, slopes, dist_scale, sink=None):
    n, seq_len, hkv, grp, hd = q.shape
    nb = -(-seq_len // q_block)
    lp = nb * q_block
    q = jnp.pad(q, ((0, 0), (0, lp - seq_len), (0, 0), (0, 0), (0, 0)))
    pad_k = ((0, 0), (half_win, half_win + lp - seq_len), (0, 0), (0, 0))
    kp = jnp.pad(k, pad_k)
    vp = jnp.pad(v, pad_k)
    span = q_block + 2 * half_win
    starts = jnp.arange(nb) * q_block
    idx = starts[:, None] + jnp.arange(span)[None]
    kpos = idx - half_win
    qpos = starts[:, None] + jnp.arange(q_block)[None]
    dist = jnp.abs(qpos[:, :, None] - kpos[:, None, :])
    valid = (dist <= half_win) & (kpos[:, None, :] >= 0) & (kpos[:, None, :] < seq_len)
    kb = kp[:, idx]
    vb = vp[:, idx]
    qb = q.reshape(n, nb, q_block, hkv, grp, hd)
    s = jnp.einsum('nbqhgd,nbkhd->nbhgqk', qb, kb).astype(jnp.float32) * (hd ** -0.5)
    bias = slopes.astype(jnp.float32)[None, :, :, None, None] * (dist * dist_scale).astype(jnp.float32)[:, None, None]
    s = jnp.where(valid[:, None, None], s - bias, NEG_INF)
    m = jnp.max(s, axis=-1)
    if sink is not None:
        sk = sink.astype(jnp.float32)[None, None, :, :, None]
        m = jnp.maximum(m, sk)
    p = jnp.exp(s - m[..., None])
    denom = jnp.sum(p, axis=-1)
    if sink is not None:
        denom = denom + jnp.exp(sk - m)
    o = jnp.einsum('nbhgqk,nbkhd->nbqhgd', p.astype(v.dtype), vb).astype(jnp.float32)
    o = (o / jnp.moveaxis(denom, -1, 2)[..., None]).astype(v.dtype)
    lse = jnp.moveaxis(m + jnp.log(denom), -1, 2)
    o = o.reshape(n, lp, hkv, grp, hd)[:, :seq_len]
    lse = lse.reshape(n, lp, hkv, grp)[:, :seq_len]
    return o, lse


def mixer_a(q, k, v, q_gain, k_gain):
    b, s, _ = q.shape
    grp = A_Q_HEADS // A_KV_HEADS
    q = rms_norm(q.reshape(b, s, A_KV_HEADS, grp, HEAD_DIM), q_gain)
    k = rms_norm(k.reshape(b, s, A_KV_HEADS, HEAD_DIM), k_gain)
    v = v.reshape(b, s, A_KV_HEADS, HEAD_DIM)
    tables = axial_rope_tables(s)
    q = axial_rope(q, *tables)
    k = axial_rope(k, *tables)
    nb = s // Q_BLOCK
    qb = jnp.moveaxis(q.reshape(b, nb, Q_BLOCK, A_KV_HEADS, grp, HEAD_DIM), 1, 0)

    def one_block(qblk):
        sc = jnp.einsum('bqhgd,bkhd->bhgqk', qblk, k).astype(jnp.float32) * (HEAD_DIM ** -0.5)
        p = jax.nn.softmax(sc, axis=-1).astype(v.dtype)
        return jnp.einsum('bhgqk,bkhd->bqhgd', p, v)

    o = lax.map(one_block, qb)
    return jnp.moveaxis(o, 0, 1).reshape(b, s, A_Q_W)


def mixer_b(q, k, v):
    b, s, _ = q.shape
    hg = B_HEADS_PER_GROUP
    q = q.reshape(b, s, B_HEADS, HEAD_DIM)
    k = k.reshape(b, s, B_HEADS, HEAD_DIM)
    v = v.reshape(b, s, B_HEADS, HEAD_DIM)
    slopes = alibi_slopes(B_HEADS)
    outs, lses = [], []
    for g, (win, r) in enumerate(B_GROUPS):
        lo, hi = g * hg, (g + 1) * hg
        sub = s // r

        def strided(t):
            return t[:, :, lo:hi].reshape(b, sub, r, hg, HEAD_DIM).transpose(0, 2, 1, 3, 4).reshape(b * r, sub, hg, HEAD_DIM)

        o, lse = banded_attention(strided(q)[:, :, :, None], strided(k), strided(v),
                                  (win // 2) // r, B_Q_BLOCK, slopes[lo:hi][:, None], r)
        outs.append(o.reshape(b, r, sub, hg, HEAD_DIM).transpose(0, 2, 1, 3, 4).reshape(b, s, hg, HEAD_DIM))
        lses.append(lse.reshape(b, r, sub, hg).transpose(0, 2, 1, 3).reshape(b, s, hg))
    o = jnp.stack(outs, axis=2)
    w = jax.nn.softmax(jnp.stack(lses, axis=2), axis=2)
    return (o * w[..., None].astype(o.dtype)).reshape(b, s, B_W)


def mixer_c(q, k, v, sink):
    b, s, _ = q.shape
    grp = C_Q_HEADS // C_KV_HEADS
    q = q.reshape(b, s, C_KV_HEADS, grp, HEAD_DIM)
    k = k.reshape(b, s, C_KV_HEADS, HEAD_DIM)
    v = v.reshape(b, s, C_KV_HEADS, HEAD_DIM)
    slopes = alibi_slopes(C_Q_HEADS).reshape(C_KV_HEADS, grp)
    o, _ = banded_attention(q, k, v, C_HALF_WINDOW, Q_BLOCK, slopes, 1, sink)
    return o.reshape(b, s, C_Q_W)


def peer(h, w_q, sub_keys, u, v):
    b, s, d = h.shape
    t = b * s
    xt = h.reshape(t, d)
    q = (xt @ w_q).reshape(t, PEER_HEADS, 2, PEER_QUERY_DIM // 2)
    sc = jnp.einsum('thpd,pkd->thpk', q, sub_keys).astype(jnp.float32)
    vals, idx = lax.top_k(sc, PEER_TOPK)
    cand = vals[:, :, 0, :, None] + vals[:, :, 1, None, :]
    cand_idx = idx[:, :, 0, :, None] * PEER_N_KEYS + idx[:, :, 1, None, :]
    top_v, pos = lax.top_k(cand.reshape(t, PEER_HEADS, PEER_TOPK * PEER_TOPK), PEER_TOPK)
    expert = jnp.take_along_axis(cand_idx.reshape(t, PEER_HEADS, -1), pos, axis=-1)
    gate = jax.nn.softmax(top_v, axis=-1).astype(h.dtype)
    expert = expert.reshape(t, PEER_HEADS * PEER_TOPK)
    gate = gate.reshape(t, PEER_HEADS * PEER_TOPK)
    nc = t // PEER_TOKEN_CHUNK

    def chunk(args):
        xc, ec, gc = args
        act = jax.nn.gelu(jnp.einsum('cd,ckd->ck', xc, u[ec]), approximate=False)
        return jnp.einsum('ck,ckd->cd', gc * act, v[ec])

    out = lax.map(chunk, (xt.reshape(nc, PEER_TOKEN_CHUNK, d),
                          expert.reshape(nc, PEER_TOKEN_CHUNK, -1),
                          gate.reshape(nc, PEER_TOKEN_CHUNK, -1)))
    return out.reshape(b, s, d)


def setup_inputs(seed: int = 0) -> dict:
    key = jax.random.key(seed)
    ks = jax.random.split(key, 24)
    L, D = DEPTH, D_MODEL

    def nrm(k, shape, scale):
        return jax.random.normal(k, shape, jnp.float32) * scale

    return {
        "x": nrm(ks[0], (BATCH, SEQ, D), 1.0),
        "c": nrm(ks[1], (BATCH, D), 1.0),
        "w_mod": nrm(ks[2], (L, D, 6 * D), 0.1 * D ** -0.5),
        "b_mod": nrm(ks[3], (L, 6 * D), 0.02),
        "w_in": nrm(ks[4], (L, D, IN_WIDTH), D ** -0.5),
        "a_q_gain": 1.0 + nrm(ks[5], (L, HEAD_DIM), 0.02),
        "a_k_gain": 1.0 + nrm(ks[6], (L, HEAD_DIM), 0.02),
        "c_sink": nrm(ks[7], (L, C_KV_HEADS, C_Q_HEADS // C_KV_HEADS), 0.5),
        "w_pa": nrm(ks[8], (L, A_Q_W, D), DEEPNORM_BETA * A_Q_W ** -0.5),
        "w_pb": nrm(ks[9], (L, B_W, D), DEEPNORM_BETA * B_W ** -0.5),
        "w_pc": nrm(ks[10], (L, C_Q_W, D), DEEPNORM_BETA * C_Q_W ** -0.5),
        "w_o": nrm(ks[11], (L, D, D), DEEPNORM_BETA * D ** -0.5),
        "ln1_g": 1.0 + nrm(ks[12], (L, D), 0.02),
        "ln1_b": nrm(ks[13], (L, D), 0.02),
        "peer_wq": nrm(ks[14], (L, D, PEER_HEADS * PEER_QUERY_DIM), D ** -0.5),
        "peer_keys": nrm(ks[15], (L, 2, PEER_N_KEYS, PEER_QUERY_DIM // 2), (PEER_QUERY_DIM // 2) ** -0.5),
        "peer_u": nrm(ks[16], (L, PEER_N_EXPERTS, D), D ** -0.5),
        "peer_v": nrm(ks[17], (L, PEER_N_EXPERTS, D), DEEPNORM_BETA * PEER_HEADS ** -0.5),
        "ln2_g": 1.0 + nrm(ks[18], (L, D), 0.02),
        "ln2_b": nrm(ks[19], (L, D), 0.02),
    }


def reference(x, c, w_mod, b_mod, w_in, a_q_gain, a_k_gain, c_sink, w_pa, w_pb, w_pc, w_o,
              ln1_g, ln1_b, peer_wq, peer_keys, peer_u, peer_v, ln2_g, ln2_b):
    b, s, d = x.shape
    split_at = np.cumsum(IN_SPLITS)[:-1].tolist()
    for l in range(DEPTH):
        mod = (c @ w_mod[l] + b_mod[l])[:, None, :]
        sh_a, sc_a, g_a, sh_f, sc_f, g_f = jnp.split(mod, 6, axis=-1)
        h = x * (1.0 + sc_a) + sh_a
        qa, ka, va, qb, kb, vb, qc, kc, vc, gl = jnp.split(h @ w_in[l], split_at, axis=-1)
        ya = mixer_a(qa, ka, va, a_q_gain[l], a_k_gain[l]) @ w_pa[l]
        yb = mixer_b(qb, kb, vb) @ w_pb[l]
        yc = mixer_c(qc, kc, vc, c_sink[l]) @ w_pc[l]
        gates = jax.nn.sigmoid(gl.reshape(b, s, N_BRANCHES, d))
        merged = gates[:, :, 0] * ya + gates[:, :, 1] * yb + gates[:, :, 2] * yc
        y = merged @ w_o[l]
        x = layer_norm(DEEPNORM_ALPHA * x + (1.0 + g_a) * y, ln1_g[l], ln1_b[l])
        h = x * (1.0 + sc_f) + sh_f
        y = peer(h, peer_wq[l], peer_keys[l], peer_u[l], peer_v[l])
        x = layer_norm(DEEPNORM_ALPHA * x + (1.0 + g_f) * y, ln2_g[l], ln2_b[l])
    return x
```

```python
import numpy as np
from contextlib import ExitStack
import concourse.bass as bass
import concourse.mybir as mybir
from concourse.bass_utils import run_bass_kernel_spmd

F32 = mybir.dt.float32
BF16 = mybir.dt.bfloat16
I32 = mybir.dt.int32
U32 = mybir.dt.uint32
ALU = mybir.AluOpType
AF = mybir.ActivationFunctionType
AX = mybir.AxisListType

ENGS = ("pe", "act", "dve", "pool", "sp")
NDMASEM = 12


class Op:
    __slots__ = ("eng", "fn", "deps", "isdma", "signal", "sem", "val", "guard")

    def __init__(self, eng, fn, isdma):
        self.eng = eng
        self.fn = fn
        self.isdma = isdma
        self.deps = ()
        self.signal = False
        self.sem = None
        self.val = 0
        self.guard = 0


class MK:
    def __init__(self, nc):
        self.nc = nc
        self.ops = {e: [] for e in ENGS}
        self.lastw = {}
        self.readers = {}
        self.stack = ExitStack()
        self.nsb = 0
        self.out_ops = []
        self.since_bar = []
        self.last = {}
        self.arena = None
        self.apos = 0
        self.banks = None

    ARENA_BYTES = 204 * 1024

    def init_mem(self):
        self.arena = self.stack.enter_context(
            self.nc.sbuf_tensor("arena", [128, self.ARENA_BYTES // 4], F32))
        self.banks = [self.stack.enter_context(self.nc.psum_tensor(f"bank{i}", [128, 512], F32))
                      for i in range(8)]

    def alloc(self, shape, dt):
        esz = 4 if dt in (F32, I32, U32) else 2
        n = 1
        for x in shape[1:]:
            n *= x
        nbytes = (n * esz + 63) // 64 * 64
        assert self.apos + nbytes <= self.ARENA_BYTES, (self.apos, nbytes)
        a = self.arena[0:shape[0], self.apos // 4:(self.apos + nbytes) // 4]
        self.apos += nbytes
        if esz == 2:
            a = a.bitcast(dt)[:, 0:n]
        else:
            a = a[:, 0:n]
            if dt != F32:
                a = a.bitcast(dt)
        if len(shape) == 3:
            a = a.rearrange("p (a b) -> p a b", a=shape[1])
        elif len(shape) == 4:
            a = a.rearrange("p (a b c) -> p a b c", a=shape[1], b=shape[2])
        return a

    def scope(self):
        mk = self

        class _S:
            def __enter__(s):
                s.pos = mk.apos

            def __exit__(s, *a):
                mk.barrier()
                mk.apos = s.pos
        return _S()

    def barrier(self):
        deps = [o for o in self.last.values()] + [o for o in self.since_bar if o.isdma]
        self.since_bar = []
        for e in ENGS:
            o = Op(e, None, False)
            o.deps = list(deps)
            self.ops[e].append(o)

    def sbuf(self, shape, dt, name=None):
        self.nsb += 1
        return self.stack.enter_context(
            self.nc.sbuf_tensor(name or f"sb{self.nsb}", list(shape), dt))

    def psum(self, shape, dt, name=None):
        self.nsb += 1
        return self.stack.enter_context(
            self.nc.psum_tensor(name or f"ps{self.nsb}", list(shape), dt))

    def op(self, eng, fn, reads=(), writes=(), dma=False, out=False):
        o = Op(eng, fn, dma)
        deps = set()
        for t in reads:
            w = self.lastw.get(t)
            if w is not None:
                deps.add(w)
        for t in writes:
            w = self.lastw.get(t)
            if w is not None:
                deps.add(w)
            deps.update(self.readers.get(t, ()))
        deps.discard(o)
        o.deps = [d for d in deps if not (d.eng == "pe" and eng == "pe" and not d.isdma and not dma)]
        for t in reads:
            self.readers.setdefault(t, []).append(o)
        for t in writes:
            self.lastw[t] = o
            self.readers[t] = []
        self.ops[eng].append(o)
        self.since_bar.append(o)
        if not dma:
            self.last[eng] = o
        if out:
            self.out_ops.append(o)
        return o

    def dma(self, eng, out, in_, reads=(), writes=(), final=False, **kw):
        return self.op(eng, lambda e: e.dma_start(out=out, in_=in_, **kw), reads, writes,
                       dma=True, out=final)

    def finalize(self):
        nc = self.nc
        for e in ENGS:
            for o in self.ops[e]:
                for d in o.deps:
                    d.signal = True
        st = self.stack
        esem = {e: st.enter_context(nc.semaphore(f"s_{e}")) for e in ENGS}
        dsem = {e: [st.enter_context(nc.semaphore(f"d_{e}{i}")) for i in range(NDMASEM)]
                for e in ("act", "pool", "sp")}
        final_waits = []
        for e in ENGS:
            cnt = 0
            dcnt = 0
            dval = [0] * NDMASEM
            for o in self.ops[e]:
                if o.isdma:
                    k = dcnt % NDMASEM
                    dcnt += 1
                    o.sem = dsem[e][k]
                    o.guard = dval[k]
                    dval[k] += 16
                    o.val = dval[k]
                elif o.signal:
                    cnt += 1
                    o.sem = esem[e]
                    o.val = cnt
            if e in dsem:
                final_waits += [(dsem[e][k], dval[k]) for k in range(NDMASEM) if dval[k] > 0]
        self.maxcnt = {e: 0 for e in ENGS}

        def emit(e, eng):
            waited = {}

            def wait(sem, val):
                if val <= 0:
                    return
                key = id(sem)
                if waited.get(key, 0) >= val:
                    return
                eng.wait_ge(sem, val)
                waited[key] = val

            for o in self.ops[e]:
                for d in o.deps:
                    wait(d.sem, d.val)
                if o.fn is None:
                    continue
                if o.isdma:
                    wait(o.sem, o.guard)
                ins = o.fn(eng)
                if o.isdma:
                    ins.then_inc(o.sem, 16)
                elif o.signal:
                    ins.then_inc(o.sem, 1)
            if e == "sp":
                for sem, val in final_waits:
                    wait(sem, val)

        with nc.Block() as block:
            @block.tensor
            def _(eng):
                emit("pe", eng)

            @block.scalar
            def _(eng):
                emit("act", eng)

            @block.vector
            def _(eng):
                emit("dve", eng)

            @block.gpsimd
            def _(eng):
                emit("pool", eng)

            @block.sync
            def _(eng):
                emit("sp", eng)
        self.stack.close()


HD = 128
QSCALE = HD ** -0.5
QK_EPS = 1e-6
CH_TYPES = (["qa"] * 8 + ["ka"] * 2 + ["v"] * 2 + ["q"] * 6 + ["k"] * 6 + ["v"] * 6 +
            ["q"] * 8 + ["k"] * 2 + ["v"] * 2 + ["g"] * 48)


def chunk_dest():
    dest = []
    qi = ki = vi = gi = 0
    for t in CH_TYPES:
        if t in ("qa", "q"):
            dest.append((t, qi)); qi += 1
        elif t in ("ka", "k"):
            dest.append((t, ki)); ki += 1
        elif t == "v":
            dest.append((t, vi)); vi += 1
        else:
            dest.append((t, gi)); gi += 1
    return dest


def stage_a(mk, d, pfx="A"):
    with mk.scope():
        _stage_a_body(mk, d, pfx)


def _stage_a_body(mk, d, pfx):
    T = lambda *a: (pfx,) + a
    hT = mk.alloc([128, 16, 1024], BF16)
    xst = [mk.alloc([128, 1024], F32) for _ in range(3)]
    modT = mk.alloc([128, 6, 16], F32)
    sc1 = mk.alloc([128, 16], F32)
    qg = mk.alloc([128, 1], F32)
    kg = mk.alloc([128, 1], F32)
    cos = mk.alloc([128, 1024], F32)
    sin = mk.alloc([128, 1024], F32)
    rmT = mk.alloc([128, 128], BF16)
    onesm = mk.alloc([128, 128], BF16)
    wbuf = [mk.alloc([128, 16, 512], BF16) for _ in range(3)]
    ostage = [mk.alloc([128, 1024], BF16) for _ in range(3)]
    vstage = mk.alloc([128, 8, 1280], BF16)
    sq = [mk.alloc([128, 512], BF16) for _ in range(2)]
    rstd = [mk.alloc([128, 512], F32) for _ in range(2)]
    qn = [mk.alloc([128, 512], BF16) for _ in range(2)]
    t1 = [mk.alloc([128, 512], F32) for _ in range(2)]
    t2 = [mk.alloc([128, 512], F32) for _ in range(2)]
    psm = mk.banks[0:4]
    pss = mk.banks[4:6]
    psr = mk.banks[6:8]

    mk.dma("sp", modT[:], d["modT"], writes=[T("modT")])
    mk.dma("sp", qg[:], d["qg"], writes=[T("qg")])
    mk.dma("sp", kg[:], d["kg"], writes=[T("kg")])
    mk.dma("sp", cos[:], d["cos"], writes=[T("cos")])
    mk.dma("sp", sin[:], d["sin"], writes=[T("sin")])
    mk.dma("sp", rmT[:], d["rmT"], writes=[T("rmT")])
    mk.dma("sp", onesm[:], d["onesm"], writes=[T("onesm")])
    epsb = mk.alloc([128, 1], F32)
    mk.op("dve", lambda e: e.memset(epsb[:], QK_EPS), writes=[T("epsb")])
    mk.op("dve", lambda e: e.tensor_scalar(out=sc1[:], in0=modT[:, 1, :], scalar1=1.0, scalar2=None,
                                           op0=ALU.add), reads=[T("modT")], writes=[T("sc1")])
    mk.op("dve", lambda e: e.tensor_scalar(out=qg[:], in0=qg[:], scalar1=QSCALE, scalar2=None,
                                           op0=ALU.mult), reads=[T("qg")], writes=[T("qg")])
    xv = d["xT"].rearrange("(kc p) t -> kc p t", p=128)
    for kc in range(16):
        xs = xst[kc % 3]
        mk.dma("sp", xs[:], xv[kc], writes=[T("xst", kc % 3)])
        mk.op("act", lambda e, xs=xs, kc=kc: e.activation(
            out=hT[:, kc, :], in_=xs[:], func=AF.Identity, bias=modT[:, 0, kc:kc + 1],
            scale=sc1[:, kc:kc + 1]),
            reads=[T("xst", kc % 3), T("sc1"), T("modT")], writes=[T("hT", kc)])

    dest = chunk_dest()
    wv = d["w_in"].rearrange("(kc p) n -> p kc n", p=128)
    nblk = (90 + 3) // 4
    pm_i = 0
    ps_i = 0
    os_i = 0
    vwrites = []
    hT_all = [T("hT", kc) for kc in range(16)]
    for b in range(nblk):
        c0 = b * 4
        nch = min(4, 90 - c0)
        wb = wbuf[b % 3]
        wtok = T("w", b % 3)
        for hh in range(2):
            mk.dma("pool", wb[:, hh * 8:(hh + 1) * 8, 0:nch * 128],
                   wv[:, hh * 8:(hh + 1) * 8, c0 * 128:(c0 + nch) * 128], writes=[(wtok, hh)])
        wreads = [(wtok, 0), (wtok, 1)]
        for ci in range(nch):
            c = c0 + ci
            kind, di = dest[c]
            wc = lambda kc, ci=ci, wb=wb: wb[:, kc, ci * 128:(ci + 1) * 128]
            if kind == "v":
                for th in range(2):
                    pm = psm[pm_i % 4]
                    ptok = T("psm", pm_i % 4)
                    pm_i += 1
                    for tt in range(4):
                        tg = th * 4 + tt
                        for kc in range(16):
                            mk.op("pe", lambda e, pm=pm, tt=tt, tg=tg, kc=kc, wc=wc: e.matmul(
                                pm[:, tt * 128:(tt + 1) * 128], hT[:, kc, tg * 128:(tg + 1) * 128], wc(kc),
                                start=(kc == 0), stop=(kc == 15)),
                                reads=hT_all + wreads, writes=[ptok])
                    vt = T("vst", di, th)
                    mk.op("act", lambda e, pm=pm, th=th, di=di: e.activation(
                        out=vstage[:, th * 4:(th + 1) * 4, di * 128:(di + 1) * 128],
                        in_=pm[:].rearrange("p (a b) -> p a b", a=4), func=AF.Copy),
                        reads=[ptok], writes=[vt])
                    vwrites.append(vt)
                continue
            ost = ostage[os_i % 3]
            otok = T("ost", os_i % 3)
            os_i += 1
            for th in range(2):
                pm = psm[pm_i % 4]
                ptok = T("psm", pm_i % 4)
                pm_i += 1
                for kc in range(16):
                    mk.op("pe", lambda e, pm=pm, th=th, kc=kc, wc=wc: e.matmul(
                        pm[:], wc(kc), hT[:, kc, th * 512:(th + 1) * 512],
                        start=(kc == 0), stop=(kc == 15)),
                        reads=hT_all + wreads, writes=[ptok])
                osl = ost[:, th * 512:(th + 1) * 512]
                if kind == "q":
                    mk.op("act", lambda e, pm=pm, osl=osl: e.activation(
                        out=osl, in_=pm[:], func=AF.Copy, scale=QSCALE),
                        reads=[ptok], writes=[(otok, th)])
                elif kind == "k":
                    mk.op("act", lambda e, pm=pm, osl=osl: e.activation(
                        out=osl, in_=pm[:], func=AF.Copy), reads=[ptok], writes=[(otok, th)])
                elif kind == "g":
                    mk.op("act", lambda e, pm=pm, osl=osl: e.activation(
                        out=osl, in_=pm[:], func=AF.Sigmoid), reads=[ptok], writes=[(otok, th)])
                else:
                    j = ps_i % 2
                    ps_i += 1
                    gain = qg if kind == "qa" else kg
                    gtok = T("qg") if kind == "qa" else T("kg")
                    mk.op("act", lambda e, pm=pm, j=j: e.activation(
                        out=sq[j][:], in_=pm[:], func=AF.Square), reads=[ptok], writes=[T("sq", j)])
                    mk.op("pe", lambda e, j=j: e.matmul(pss[j][:], onesm[:], sq[j][:], start=True, stop=True),
                          reads=[T("onesm"), T("sq", j)], writes=[T("pss", j)])
                    mk.op("act", lambda e, j=j: e.activation(
                        out=t1[j][:], in_=pss[j][:], func=AF.Ln, bias=epsb[:, 0:1]),
                        reads=[T("pss", j), T("epsb")], writes=[T("t1", j)])
                    mk.op("act", lambda e, j=j: e.activation(
                        out=rstd[j][:], in_=t1[j][:], func=AF.Exp, scale=-0.5),
                        reads=[T("t1", j)], writes=[T("rstd", j)])
                    mk.op("dve", lambda e, j=j, pm=pm, gain=gain: e.scalar_tensor_tensor(
                        out=qn[j][:], in0=pm[:], scalar=gain[:, 0:1], in1=rstd[j][:], op0=ALU.mult, op1=ALU.mult),
                        reads=[ptok, gtok, T("rstd", j)], writes=[T("qn", j)])
                    mk.op("pe", lambda e, j=j: e.matmul(psr[j][:], rmT[:], qn[j][:], start=True, stop=True),
                          reads=[T("rmT"), T("qn", j)], writes=[T("psr", j)])
                    mk.op("pool", lambda e, j=j, th=th: e.tensor_tensor(
                        out=t1[j][:], in0=qn[j][:], in1=cos[:, th * 512:(th + 1) * 512], op=ALU.mult),
                        reads=[T("qn", j), T("cos")], writes=[T("t1", j)])
                    mk.op("dve", lambda e, j=j, th=th: e.tensor_tensor(
                        out=t2[j][:], in0=psr[j][:], in1=sin[:, th * 512:(th + 1) * 512], op=ALU.mult),
                        reads=[T("psr", j), T("sin")], writes=[T("t2", j)])
                    mk.op("dve", lambda e, j=j, osl=osl: e.tensor_tensor(
                        out=osl, in0=t1[j][:], in1=t2[j][:], op=ALU.add),
                        reads=[T("t1", j), T("t2", j)], writes=[(otok, th)])
            if kind in ("q", "qa"):
                dst = d["qT"][di]
            elif kind in ("k", "ka"):
                dst = d["kT"][di]
            else:
                dst = d["gates"][di]
            mk.dma("sp", dst, ost[:], reads=[(otok, 0), (otok, 1)], writes=[T("dram", kind, di)], final=True)
    mk.dma("sp", d["v"].rearrange("(tt p) n -> p tt n", p=128), vstage[:], reads=vwrites,
           writes=[T("dram", "v")], final=True)


NEG = -30000.0
B_GROUPS = ((128, 1), (512, 4), (2048, 16))
B_REL = (1, 2, 8)
B_OFF = (0, 3, 8)
WIN = 24


def attn_a(mk, d, oT, pfx="AA"):
    T = lambda *a: (pfx,) + a
    with mk.scope():
        q = mk.alloc([128, 8, 1024], BF16)
        kT = mk.alloc([128, 2, 4096], BF16)
        V = mk.alloc([128, 32, 256], BF16)
        ones = mk.alloc([128, 128], BF16)
        PT = [mk.alloc([128, 512], BF16) for _ in range(3)]
        rec = [mk.alloc([128, 512], F32) for _ in range(2)]
        mk.dma("sp", ones, d["ones"], writes=[T("ones")])
        for g in range(2):
            mk.dma("sp", kT[:, g, :], d["kTA"][g], writes=[T("k", g)])
        vv = d["vA"].rearrange("(kt p) c -> p kt c", p=128)
        for hh in range(2):
            mk.dma("act", V[:, hh * 16:(hh + 1) * 16, :], vv[:, hh * 16:(hh + 1) * 16, :], writes=[T("v", hh)])
        for h in range(8):
            mk.dma("sp", q[:, h, :], d["qT"][h], writes=[T("q", h)])
        si = 0
        it = 0
        for g in range(2):
            for hh in range(4):
                h = g * 4 + hh
                for th in range(2):
                    psO = mk.banks[3 + it % 2]
                    psD = mk.banks[5 + it % 2]
                    otok, dtok = T("psO", it % 2), T("psD", it % 2)
                    rc = rec[it % 2]
                    it += 1
                    qs = q[:, h, th * 512:(th + 1) * 512]

                    def S(kt, si):
                        b = si % 3
                        mk.op("pe", lambda e, b=b, kt=kt, g=g, qs=qs: e.matmul(
                            mk.banks[b][:], kT[:, g, kt * 128:(kt + 1) * 128], qs, start=True, stop=True),
                            reads=[T("k", g), T("q", h)], writes=[T("psS", b)])
                    S(0, si)
                    for kt in range(32):
                        b = (si + kt) % 3
                        if kt + 1 < 32:
                            S(kt + 1, si + kt + 1)
                        mk.op("act", lambda e, b=b: e.activation(out=PT[b], in_=mk.banks[b][:], func=AF.Exp),
                              reads=[T("psS", b)], writes=[T("PT", b)])
                        mk.op("pe", lambda e, b=b, kt=kt, psO=psO, g=g: e.matmul(
                            psO[:], V[:, kt, g * 128:(g + 1) * 128], PT[b], start=(kt == 0), stop=(kt == 31)),
                            reads=[T("v", kt // 16), T("PT", b)], writes=[otok])
                        mk.op("pe", lambda e, b=b, kt=kt, psD=psD: e.matmul(
                            psD[:], ones, PT[b], start=(kt == 0), stop=(kt == 31)),
                            reads=[T("ones"), T("PT", b)], writes=[dtok])
                    si += 32
                    mk.op("dve", lambda e, psD=psD, rc=rc: e.reciprocal(out=rc, in_=psD[:]),
                          reads=[dtok], writes=[T("rec", id(rc))])
                    mk.op("dve", lambda e, psO=psO, rc=rc, h=h, th=th: e.tensor_tensor(
                        out=oT[:, h, th * 512:(th + 1) * 512], in0=psO[:], in1=rc, op=ALU.mult),
                        reads=[otok, T("rec", id(rc))], writes=[("oT", h, th)])


def attn_c(mk, d, oT, pfx="AC"):
    T = lambda *a: (pfx,) + a
    with mk.scope():
        q = mk.alloc([128, 8, 1024], BF16)
        kC = mk.alloc([128, 2, 10 * 128], BF16)
        vC = mk.alloc([128, 10, 256], BF16)
        ones = mk.alloc([128, 128], BF16)
        bias = mk.alloc([128, 3, 8, 128], F32)
        kval = mk.alloc([128, WIN], F32)
        sink = mk.alloc([128, 8], F32)
        sinkb = mk.alloc([128, 8, 128], F32)
        Sb = [mk.alloc([128, 512], F32) for _ in range(3)]
        PT = [mk.alloc([128, 512], BF16) for _ in range(3)]
        rec = [mk.alloc([128, 512], F32) for _ in range(2)]
        mk.dma("sp", ones, d["ones"], writes=[T("ones")])
        mk.dma("sp", bias, d["biasC"], writes=[T("bias")])
        mk.dma("sp", kval, d["kvalid"], writes=[T("kval")])
        mk.dma("sp", sink, d["sink"], writes=[T("sink")])
        for g in range(2):
            mk.dma("sp", kC[:, g, :], d["kTw"][6 + g][:, 7 * 128:17 * 128], writes=[T("k", g)])
        vv = d["vw"].rearrange("(kt p) c -> p kt c", p=128)
        mk.dma("act", vC, vv[:, 7:17, 768:1024], writes=[T("v")])
        for h in range(8):
            mk.dma("sp", q[:, h, :], d["qT"][14 + h], writes=[T("q", h)])
        mk.op("act", lambda e: e.activation(out=sink, in_=sink, func=AF.Exp), reads=[T("sink")], writes=[T("sink")])
        mk.op("dve", lambda e: e.tensor_copy(out=sinkb, in_=sink.unsqueeze(2).to_broadcast([128, 8, 128])),
              reads=[T("sink")], writes=[T("sinkb")])
        si = 0
        it = 0
        for qt in range(8):
            for g in range(2):
                psO = mk.banks[3 + it % 2]
                psD = mk.banks[5 + it % 2]
                otok, dtok = T("psO", it % 2), T("psD", it % 2)
                rc = rec[it % 2]
                it += 1
                qs = q[:, 4 * g:4 * g + 4, qt * 128:(qt + 1) * 128]
                for r in range(3):
                    b = (si + r) % 3
                    w = 8 + qt + r - 1
                    mk.op("pe", lambda e, b=b, w=w, qs=qs, g=g: e.matmul(
                        mk.banks[b][:].rearrange("p (a c) -> p a c", a=4), kC[:, g, (w - 7) * 128:(w - 6) * 128], qs,
                        start=True, stop=True),
                        reads=[T("k", g)] + [T("q", 4 * g + i) for i in range(4)], writes=[T("psS", b)])
                for r in range(3):
                    b = (si + r) % 3
                    w = 8 + qt + r - 1
                    mk.op("dve", lambda e, b=b, r=r, g=g: e.tensor_tensor(
                        out=Sb[b].rearrange("p (a c) -> p a c", a=4),
                        in0=mk.banks[b][:].rearrange("p (a c) -> p a c", a=4),
                        in1=bias[:, r, 4 * g:4 * g + 4, :], op=ALU.add),
                        reads=[T("psS", b), T("bias")], writes=[T("Sb", b)])
                    mk.op("act", lambda e, b=b, w=w: e.activation(
                        out=PT[b], in_=Sb[b], func=AF.Exp, bias=kval[:, w:w + 1]),
                        reads=[T("Sb", b), T("kval")], writes=[T("PT", b)])
                    mk.op("pe", lambda e, b=b, w=w, r=r, g=g, psO=psO: e.matmul(
                        psO[:], vC[:, w - 7, g * 128:(g + 1) * 128], PT[b], start=(r == 0), stop=(r == 2)),
                        reads=[T("v"), T("PT", b)], writes=[otok])
                    mk.op("pe", lambda e, b=b, r=r, psD=psD: e.matmul(
                        psD[:], ones, PT[b], start=(r == 0), stop=(r == 2)),
                        reads=[T("ones"), T("PT", b)], writes=[dtok])
                si += 3
                mk.op("dve", lambda e, psD=psD, rc=rc, g=g: e.tensor_tensor(
                    out=rc.rearrange("p (a c) -> p a c", a=4), in0=psD[:].rearrange("p (a c) -> p a c", a=4),
                    in1=sinkb[:, 4 * g:4 * g + 4, :], op=ALU.add),
                    reads=[dtok, T("sinkb")], writes=[T("rec", id(rc))])
                mk.op("dve", lambda e, rc=rc: e.reciprocal(out=rc, in_=rc),
                      reads=[T("rec", id(rc))], writes=[T("rec", id(rc))])
                mk.op("dve", lambda e, psO=psO, rc=rc, g=g, qt=qt: e.tensor_tensor(
                    out=oT[:, 14 + 4 * g:14 + 4 * g + 4, qt * 128:(qt + 1) * 128],
                    in0=psO[:].rearrange("p (a c) -> p a c", a=4),
                    in1=rc.rearrange("p (a c) -> p a c", a=4), op=ALU.mult),
                    reads=[otok, T("rec", id(rc))], writes=[("oT", 14 + 4 * g + i, qt // 4) for i in range(4)])


def attn_b(mk, d, oT, pfx="AB"):
    T = lambda *a: (pfx,) + a
    with mk.scope():
        q = mk.alloc([128, 6, 1024], BF16)
        kB = mk.alloc([128, 6, WIN * 128], BF16)
        vB = mk.alloc([128, WIN, 768], BF16)
        ones = mk.alloc([128, 128], BF16)
        bias = mk.alloc([128, 25, 256], F32)
        kval = mk.alloc([128, WIN], F32)
        Sb = [mk.alloc([128, 256], F32) for _ in range(3)]
        PT = [mk.alloc([128, 256], BF16) for _ in range(3)]
        rec = [mk.alloc([128, 256], F32) for _ in range(2)]
        mk.dma("sp", ones, d["ones"], writes=[T("ones")])
        mk.dma("sp", bias, d["biasB"], writes=[T("bias")])
        mk.dma("sp", kval, d["kvalid"], writes=[T("kval")])
        for h in range(6):
            mk.dma("sp", kB[:, h, :], d["kTw"][h], writes=[T("k", h)])
        vv = d["vw"].rearrange("(kt p) c -> p kt c", p=128)
        for hh in range(3):
            mk.dma("act", vB[:, hh * 8:(hh + 1) * 8, :], vv[:, hh * 8:(hh + 1) * 8, 0:768], writes=[T("v", hh)])
        for h in range(6):
            mk.dma("sp", q[:, h, :], d["qT"][8 + h], writes=[T("q", h)])
        si = 0
        for qt in range(8):
            psD = mk.banks[3 + qt % 2]
            psOj = [mk.banks[5], mk.banks[6]]
            dtok = T("psD", qt % 2)
            ojtok = [T("psOj", 0), T("psOj", 1)]
            rc = rec[qt % 2]
            tiles = []
            for gi in range(3):
                R = B_REL[gi]
                for rel in range(-R, R + 1):
                    tiles.append((gi, rel))
            nt = len(tiles)

            def S(i, b):
                gi, rel = tiles[i]
                w = 8 + qt + rel
                for j in range(2):
                    h = 2 * gi + j
                    mk.op("pe", lambda e, b=b, w=w, h=h, j=j, qt=qt: e.matmul(
                        mk.banks[b][:, j * 128:(j + 1) * 128], kB[:, h, w * 128:(w + 1) * 128],
                        q[:, h, qt * 128:(qt + 1) * 128], start=True, stop=True),
                        reads=[T("k", h), T("q", h)], writes=[T("psS", b)])
            S(0, si % 3)
            for i in range(nt):
                gi, rel = tiles[i]
                R = B_REL[gi]
                w = 8 + qt + rel
                b = (si + i) % 3
                if i + 1 < nt:
                    S(i + 1, (si + i + 1) % 3)
                ti = B_OFF[gi] + rel + R
                mk.op("dve", lambda e, b=b, ti=ti: e.tensor_tensor(
                    out=Sb[b], in0=mk.banks[b][:, 0:256], in1=bias[:, ti, :], op=ALU.add),
                    reads=[T("psS", b), T("bias")], writes=[T("Sb", b)])
                mk.op("act", lambda e, b=b, w=w: e.activation(
                    out=PT[b], in_=Sb[b], func=AF.Exp, bias=kval[:, w:w + 1]),
                    reads=[T("Sb", b), T("kval")], writes=[T("PT", b)])
                for j in range(2):
                    h = 2 * gi + j
                    po, ptok = psOj[j], ojtok[j]
                    c0 = gi * 128
                    mk.op("pe", lambda e, b=b, w=w, h=h, j=j, po=po, c0=c0, rel=rel, R=R: e.matmul(
                        po[:, c0:c0 + 128], vB[:, w, h * 128:(h + 1) * 128], PT[b][:, j * 128:(j + 1) * 128],
                        start=(rel == -R), stop=(rel == R)),
                        reads=[T("v", w // 8), T("PT", b)], writes=[ptok])
                mk.op("pe", lambda e, b=b, i=i, psD=psD: e.matmul(
                    psD[:, 0:256], ones, PT[b], start=(i == 0), stop=(i == nt - 1)),
                    reads=[T("ones"), T("PT", b)], writes=[dtok])
            si += nt
            mk.op("dve", lambda e, psD=psD, rc=rc: e.reciprocal(out=rc, in_=psD[:, 0:256]),
                  reads=[dtok], writes=[T("rec", qt % 2)])
            for gi in range(3):
                for j in range(2):
                    mk.op("dve", lambda e, rc=rc, gi=gi, j=j, qt=qt, po=psOj[j]: e.tensor_tensor(
                        out=oT[:, 8 + 2 * gi + j, qt * 128:(qt + 1) * 128],
                        in0=po[:, gi * 128:(gi + 1) * 128], in1=rc[:, j * 128:(j + 1) * 128], op=ALU.mult),
                        reads=[ojtok[j], T("rec", qt % 2)], writes=[("oT", 8 + 2 * gi + j, qt // 4)])


import ml_dtypes
NPBF = ml_dtypes.bfloat16


def alibi_slopes_np(n):
    return (2.0 ** (-8.0 * np.arange(1, n + 1) / n)).astype(np.float32)


def bias_tables():
    kk = np.arange(128)[:, None]
    qq = np.arange(128)[None, :]
    sl = alibi_slopes_np(8)
    bc = np.zeros((128, 3, 8, 128), np.float32)
    for r in range(3):
        delta = (r - 1) * 128 + kk - qq
        valid = np.abs(delta) <= 128
        for h in range(8):
            bc[:, r, h, :] = np.where(valid, -sl[h] * np.abs(delta).astype(np.float32), NEG)
    sl6 = alibi_slopes_np(6)
    bb = np.zeros((128, 25, 2, 128), np.float32)
    for gi, (win, r_) in enumerate(B_GROUPS):
        R = B_REL[gi]
        for rel in range(-R, R + 1):
            delta = rel * 128 + kk - qq
            valid = (np.abs(delta) <= win // 2) & (delta % r_ == 0)
            for j in range(2):
                bb[:, B_OFF[gi] + rel + R, j, :] = np.where(
                    valid, -sl6[2 * gi + j] * np.abs(delta).astype(np.float32), NEG)
    return bc, bb.reshape(128, 25, 256)


def rope_tables(pos):
    row = (pos // 64).astype(np.float32)
    col = (pos % 64).astype(np.float32)
    inv = (10000.0 ** (-np.arange(32, dtype=np.float32) / 32)).astype(np.float32)
    ar = (row[:, None] * inv).astype(np.float32)
    ac = (col[:, None] * inv).astype(np.float32)
    cos = np.concatenate([np.cos(ar), np.cos(ar), np.cos(ac), np.cos(ac)], axis=1).T
    sin = np.concatenate([np.sin(ar), np.sin(ar), np.sin(ac), np.sin(ac)], axis=1).T
    return np.ascontiguousarray(cos.astype(np.float32)), np.ascontiguousarray(sin.astype(np.float32))


def rot_matrix_T():
    R = np.zeros((128, 128), np.float32)
    for dd in range(128):
        if (dd // 32) % 2 == 0:
            R[dd, dd + 32] = -1.0
        else:
            R[dd, dd - 32] = 1.0
    return np.ascontiguousarray(R.T).astype(NPBF)


def gather_kv(kTs, vs):
    kT = np.concatenate(kTs, axis=2)
    v = np.concatenate(vs, axis=0)
    kTA = np.ascontiguousarray(kT[0:2])
    vA = np.ascontiguousarray(v[:, 0:256])
    kpad = np.zeros((8, 128, 4096 + 2048), kT.dtype)
    kpad[:, :, 1024:1024 + 4096] = kT[2:10]
    vpad = np.zeros((4096 + 2048, 1024), v.dtype)
    vpad[1024:1024 + 4096] = v[:, 256:1280]
    outs = []
    for r in range(4):
        s0 = r * 1024
        tiles = r * 8 - 8 + np.arange(WIN)
        kval = np.where((tiles >= 0) & (tiles < 32), 0.0, NEG).astype(np.float32)
        outs.append({"kTA": kTA, "vA": vA,
                     "kTw": np.ascontiguousarray(kpad[:, :, s0:s0 + WIN * 128]),
                     "vw": np.ascontiguousarray(vpad[s0:s0 + WIN * 128]),
                     "kvalid": np.ascontiguousarray(np.broadcast_to(kval[None, :], (128, WIN)))})
    return outs


DEPTH = 2
ALPHA = (2 * DEPTH) ** 0.25
LN_EPS = 1e-5


def merge_phase(mk, d, oT, merged, pfx="M"):
    T = lambda *a: (pfx,) + a
    with mk.scope():
        wblk = [mk.alloc([128, 22, 512], BF16) for _ in range(2)]
        gt = [mk.alloc([128, 3, 1024], BF16) for _ in range(2)]
        ta = [mk.alloc([128, 512], F32) for _ in range(2)]
        tb = [mk.alloc([128, 512], F32) for _ in range(2)]
        tc_ = [mk.alloc([128, 512], F32) for _ in range(2)]
        srcs = [(d["w_pa"], 8, 0), (d["w_pb"], 6, 8), (d["w_pc"], 8, 14)]
        gv = d["gates"].rearrange("(i f) p t -> f p i t", i=3)
        it = 0
        for fb in range(4):
            wb = wblk[fb % 2]
            wtok = T("w", fb % 2)
            for si_, (wd, nk, k0) in enumerate(srcs):
                mk.dma("pool", wb[:, k0:k0 + nk, :],
                       wd.rearrange("(kc p) n -> p kc n", p=128)[:, :, fb * 512:(fb + 1) * 512],
                       writes=[(wtok, si_)])
            for fi in range(4):
                f = fb * 4 + fi
                g = gt[f % 2]
                gtok = T("g", f % 2)
                mk.dma("sp", g, gv[f], writes=[gtok])
                for th in range(2):
                    j = it % 2
                    it += 1
                    ps = [mk.banks[3 * j + i] for i in range(3)]
                    ptok = [T("ps", 3 * j + i) for i in range(3)]
                    tsl = slice(th * 512, (th + 1) * 512)
                    for i, (wd, nk, k0) in enumerate(srcs):
                        for kc in range(nk):
                            mk.op("pe", lambda e, p=ps[i], wb=wb, kk=k0 + kc, fi=fi, tsl=tsl, kc=kc, nk=nk: e.matmul(
                                p[:], wb[:, kk, fi * 128:(fi + 1) * 128], oT[:, kk, tsl],
                                start=(kc == 0), stop=(kc == nk - 1)),
                                reads=[(wtok, i), ("oT", k0 + kc, th)], writes=[ptok[i]])
                    mk.op("dve", lambda e, j=j, p=ps[0], g=g, tsl=tsl: e.tensor_tensor(
                        out=ta[j], in0=p[:], in1=g[:, 0, tsl], op=ALU.mult),
                        reads=[ptok[0], gtok], writes=[T("ta", j)])
                    mk.op("dve", lambda e, j=j, p=ps[1], g=g, tsl=tsl: e.tensor_tensor(
                        out=tb[j], in0=p[:], in1=g[:, 1, tsl], op=ALU.mult),
                        reads=[ptok[1], gtok], writes=[T("tb", j)])
                    mk.op("dve", lambda e, j=j, p=ps[2], g=g, tsl=tsl: e.tensor_tensor(
                        out=tc_[j], in0=p[:], in1=g[:, 2, tsl], op=ALU.mult),
                        reads=[ptok[2], gtok], writes=[T("tc", j)])
                    mk.op("pool", lambda e, j=j: e.tensor_tensor(out=ta[j], in0=ta[j], in1=tb[j], op=ALU.add),
                          reads=[T("ta", j), T("tb", j)], writes=[T("ta", j)])
                    mk.op("pool", lambda e, j=j, f=f, tsl=tsl: e.tensor_tensor(
                        out=merged[:, f, tsl], in0=ta[j], in1=tc_[j], op=ALU.add),
                        reads=[T("ta", j), T("tc", j)], writes=[("merged", f, th)])


def rsqrt_act(mk, out, in_, epsb, toks_in, tok_out, tmp, tok_tmp):
    mk.op("act", lambda e: e.activation(out=tmp, in_=in_, func=AF.Ln, bias=epsb[:, 0:1]),
          reads=toks_in, writes=[tok_tmp])
    mk.op("act", lambda e: e.activation(out=out, in_=tmp, func=AF.Exp, scale=-0.5),
          reads=[tok_tmp], writes=[tok_out])


def wo_ln_phase(mk, d, merged, pfx="W"):
    resid_ln_phase(mk, d, pfx, merged=merged, xin="xT", xout="x1T", gidx=2, lng_k="ln1g", lnb_k="ln1b")


def resid_ln_phase(mk, d, pfx, merged=None, ytok=None, zT=None, xin="xT", xout="x1T", gidx=2,
                   lng_k="ln1g", lnb_k="ln1b"):
    T = lambda *a: (pfx,) + a
    with mk.scope():
        if zT is None:
            zT = mk.alloc([128, 16, 1024], F32)
        if merged is not None:
            wo = [mk.alloc([128, 16, 512], BF16) for _ in range(2)]
        else:
            ident = mk.alloc([128, 128], F32)
            mk.dma("sp", ident, d["identf"], writes=[T("ident")])
        xt = [mk.alloc([128, 1024], F32) for _ in range(2)]
        tt = [mk.alloc([128, 512], F32) for _ in range(2)]
        sqt = [mk.alloc([128, 512], F32) for _ in range(2)]
        modT = mk.alloc([128, 6, 16], F32)
        g1 = mk.alloc([128, 16], F32)
        lng = mk.alloc([128, 16], F32)
        lnb = mk.alloc([128, 16], F32)
        onesf = mk.alloc([128, 128], F32)
        epsb = mk.alloc([128, 1], F32)
        mean = [mk.alloc([128, 512], F32) for _ in range(2)]
        rstd = [mk.alloc([128, 512], F32) for _ in range(2)]
        mk.dma("sp", modT, d["modT"], writes=[T("modT")])
        mk.dma("sp", lng, d[lng_k], writes=[T("lng")])
        mk.dma("sp", lnb, d[lnb_k], writes=[T("lnb")])
        mk.op("dve", lambda e: e.memset(onesf, 1.0 / 2048), writes=[T("onesf")])
        mk.op("dve", lambda e: e.memset(epsb, LN_EPS), writes=[T("epsb")])
        mk.op("dve", lambda e: e.tensor_scalar(out=g1, in0=modT[:, gidx, :], scalar1=1.0, scalar2=None, op0=ALU.add),
              reads=[T("modT")], writes=[T("g1")])
        xv = d[xin].rearrange("(kc p) t -> kc p t", p=128)
        if merged is not None:
            wv = d["w_o"].rearrange("(kc p) n -> p kc n", p=128)
        it = 0
        for fb in range(4):
            if merged is not None:
                wb = wo[fb % 2]
                wtok = T("w", fb % 2)
                for hh in range(2):
                    mk.dma("pool", wb[:, hh * 8:(hh + 1) * 8, :], wv[:, hh * 8:(hh + 1) * 8, fb * 512:(fb + 1) * 512],
                           writes=[(wtok, hh)])
            for fi in range(4):
                f = fb * 4 + fi
                xs = xt[f % 2]
                xtok = T("x", f % 2)
                mk.dma("sp", xs, xv[f], writes=[xtok])
                for th in range(2):
                    j = it % 2
                    it += 1
                    ps = mk.banks[j]
                    ptok = T("ps", j)
                    tsl = slice(th * 512, (th + 1) * 512)
                    if merged is not None:
                        for kc in range(16):
                            mk.op("pe", lambda e, ps=ps, wb=wb, kc=kc, fi=fi, tsl=tsl: e.matmul(
                                ps[:], wb[:, kc, fi * 128:(fi + 1) * 128], merged[:, kc, tsl],
                                start=(kc == 0), stop=(kc == 15)),
                                reads=[(wtok, kc // 8), ("merged", kc, th)], writes=[ptok])
                    else:
                        for t4 in range(4):
                            tg = th * 4 + t4
                            mk.op("pe", lambda e, ps=ps, t4=t4, tg=tg, f=f: e.transpose(
                                ps[:, t4 * 128:(t4 + 1) * 128], ytok[:, tg, f * 128:(f + 1) * 128], ident),
                                reads=[T("ident"), ("ytok", tg)], writes=[ptok])
                    mk.op("act", lambda e, ps=ps, j=j, f=f: e.activation(
                        out=tt[j], in_=ps[:], func=AF.Identity, scale=g1[:, f:f + 1]),
                        reads=[ptok, T("g1")], writes=[T("tt", j)])
                    mk.op("dve", lambda e, j=j, xs=xs, f=f, tsl=tsl: e.scalar_tensor_tensor(
                        out=zT[:, f, tsl], in0=xs[:, tsl], scalar=ALPHA, in1=tt[j], op0=ALU.mult, op1=ALU.add),
                        reads=[xtok, T("tt", j)], writes=[T("z", f, th)])
                    mk.op("act", lambda e, j=j, f=f, tsl=tsl: e.activation(out=sqt[j], in_=zT[:, f, tsl], func=AF.Square),
                          reads=[T("z", f, th)], writes=[T("sq", j)])
                    mk.op("pe", lambda e, f=f, th=th, tsl=tsl: e.matmul(
                        mk.banks[4 + th][:], onesf, zT[:, f, tsl], start=(f == 0), stop=(f == 15)),
                        reads=[T("onesf"), T("z", f, th)], writes=[T("psum", th)])
                    mk.op("pe", lambda e, f=f, th=th, j=j: e.matmul(
                        mk.banks[6 + th][:], onesf, sqt[j], start=(f == 0), stop=(f == 15)),
                        reads=[T("onesf"), T("sq", j)], writes=[T("pssq", th)])
        for th in range(2):
            mk.op("dve", lambda e, th=th: e.tensor_copy(out=mean[th], in_=mk.banks[4 + th][:]),
                  reads=[T("psum", th)], writes=[T("mean", th)])
            mk.op("dve", lambda e, th=th: e.tensor_tensor(out=tt[th], in0=mean[th], in1=mean[th], op=ALU.mult),
                  reads=[T("mean", th)], writes=[T("tt", th)])
            mk.op("dve", lambda e, th=th: e.tensor_tensor(out=tt[th], in0=mk.banks[6 + th][:], in1=tt[th], op=ALU.subtract),
                  reads=[T("pssq", th), T("tt", th)], writes=[T("tt", th)])
            rsqrt_act(mk, rstd[th], tt[th], epsb, [T("tt", th), T("epsb")], T("rstd", th), sqt[th], T("sq", th))
        xo = d[xout].rearrange("(kc p) t -> kc p t", p=128)
        for f in range(16):
            for th in range(2):
                tsl = slice(th * 512, (th + 1) * 512)
                mk.op("dve", lambda e, f=f, th=th, tsl=tsl: e.tensor_tensor(
                    out=zT[:, f, tsl], in0=zT[:, f, tsl], in1=mean[th], op=ALU.subtract),
                    reads=[T("z", f, th), T("mean", th)], writes=[T("z", f, th)])
                mk.op("pool", lambda e, f=f, th=th, tsl=tsl: e.tensor_tensor(
                    out=zT[:, f, tsl], in0=zT[:, f, tsl], in1=rstd[th], op=ALU.mult),
                    reads=[T("z", f, th), T("rstd", th)], writes=[T("z", f, th)])
                mk.op("act", lambda e, f=f, tsl=tsl: e.activation(
                    out=zT[:, f, tsl], in_=zT[:, f, tsl], func=AF.Identity, bias=lnb[:, f:f + 1], scale=lng[:, f:f + 1]),
                    reads=[T("z", f, th), T("lng"), T("lnb")], writes=[T("z", f, th)])
            mk.dma("sp", xo[f], zT[:, f, :], reads=[T("z", f, 0), T("z", f, 1)], writes=[(xout, f)], final=True)


def peer_phase(mk, d, pfx="P"):
    T = lambda *a: (pfx,) + a
    h2tok = mk.alloc([128, 8, 2048], F32)
    eid = mk.alloc([128, 8, 128], I32)
    gate = mk.alloc([128, 8, 128], F32)
    with mk.scope():
        h2b = mk.alloc([128, 16, 1024], BF16)
        modT = mk.alloc([128, 6, 16], F32)
        sc1 = mk.alloc([128, 16], F32)
        ident = mk.alloc([128, 128], F32)
        mk.dma("sp", modT, d["modT"], writes=[T("modT")])
        mk.dma("sp", ident, d["identf"], writes=[T("ident")])
        mk.op("dve", lambda e: e.tensor_scalar(out=sc1, in0=modT[:, 4, :], scalar1=1.0, scalar2=None, op0=ALU.add),
              reads=[T("modT")], writes=[T("sc1")])
        with mk.scope():
            xs = [mk.alloc([128, 1024], F32) for _ in range(2)]
            hf = [mk.alloc([128, 1024], F32) for _ in range(2)]
            xv = d["x1T"].rearrange("(kc p) t -> kc p t", p=128)
            for kc in range(16):
                x_ = xs[kc % 2]
                h_ = hf[kc % 2]
                mk.dma("sp", x_, xv[kc], reads=[("x1T", kc)], writes=[T("xs", kc % 2)])
                mk.op("act", lambda e, x_=x_, h_=h_, kc=kc: e.activation(
                    out=h_, in_=x_, func=AF.Identity, bias=modT[:, 3, kc:kc + 1], scale=sc1[:, kc:kc + 1]),
                    reads=[T("xs", kc % 2), T("sc1"), T("modT")], writes=[T("hf", kc % 2)])
                mk.op("pool", lambda e, h_=h_, kc=kc: e.tensor_copy(out=h2b[:, kc, :], in_=h_),
                      reads=[T("hf", kc % 2)], writes=[T("h2b", kc)])
                for half in range(2):
                    pb = mk.banks[(kc * 2 + half) % 4]
                    ptok = T("pst", (kc * 2 + half) % 4)
                    for t4 in range(4):
                        tg = half * 4 + t4
                        mk.op("pe", lambda e, pb=pb, t4=t4, tg=tg, h_=h_: e.transpose(
                            pb[:, t4 * 128:(t4 + 1) * 128], h_[:, tg * 128:(tg + 1) * 128], ident),
                            reads=[T("hf", kc % 2), T("ident")], writes=[ptok])
                    mk.op("dve", lambda e, pb=pb, half=half, kc=kc: e.tensor_copy(
                        out=h2tok[:, half * 4:(half + 1) * 4, kc * 128:(kc + 1) * 128],
                        in_=pb[:].rearrange("p (a b) -> p a b", a=4)),
                        reads=[ptok], writes=[T("h2tok", half)])
        qpT = mk.alloc([128, 16, 512], F32)
        wq = [mk.alloc([128, 16, 256], BF16) for _ in range(2)]
        keysT = mk.alloc([128, 2, 128], F32)
        scsb = [mk.alloc([128, 16, 128], F32) for _ in range(2)]
        work = mk.alloc([128, 256], F32)
        vals = mk.alloc([128, 16, 16], F32)
        idxu = mk.alloc([128, 16, 16], U32)
        idf = mk.alloc([128, 16, 16], F32)
        idfa = mk.alloc([128, 16, 16], F32)
        cand = mk.alloc([128, 8, 256], F32)
        cidx = mk.alloc([128, 8, 256], F32)
        tv = mk.alloc([128, 8, 16], F32)
        junk = mk.alloc([128, 256], F32)
        eidf = mk.alloc([128, 128], F32)
        ex = mk.alloc([128, 8, 16], F32)
        ssum = mk.alloc([128, 8], F32)
        mk.dma("sp", keysT, d["keysT"], writes=[T("keysT")])
        cmask = mk.alloc([128, 256], U32)
        cpos = mk.alloc([128, 256], U32)
        mk.dma("sp", cmask, d["cmask"], writes=[T("cmask")])
        mk.dma("sp", cpos, d["cpos"], writes=[T("cpos")])
        wv = d["w_q"].rearrange("(kc p) n -> p kc n", p=128)
        vv = vals.rearrange("p (h two) k -> p h two k", two=2)
        iv = idf.rearrange("p (h two) k -> p h two k", two=2)
        it = 0
        wi = 0
        for th in range(2):
            for fb in range(8):
                wb = wq[wi % 2]
                wtok = T("wq", wi % 2)
                wi += 1
                mk.dma("pool", wb, wv[:, :, fb * 256:(fb + 1) * 256], writes=[wtok])
                for fi in range(2):
                    hp = fb * 2 + fi
                    pb = mk.banks[4 + it % 2]
                    ptok = T("psq", it % 2)
                    it += 1
                    for kc in range(16):
                        mk.op("pe", lambda e, pb=pb, wb=wb, kc=kc, fi=fi, th=th: e.matmul(
                            pb[:], wb[:, kc, fi * 128:(fi + 1) * 128], h2b[:, kc, th * 512:(th + 1) * 512],
                            start=(kc == 0), stop=(kc == 15)),
                            reads=[wtok, T("h2b", kc)], writes=[ptok])
                    mk.op("act", lambda e, pb=pb, hp=hp: e.activation(out=qpT[:, hp, :], in_=pb[:], func=AF.Copy),
                          reads=[ptok], writes=[T("qpT", hp)])
            for t4 in range(4):
                tt = th * 4 + t4
                sb_ = scsb[tt % 2]
                for hp in range(16):
                    bk = mk.banks[hp // 4]
                    mk.op("pe", lambda e, bk=bk, hp=hp, t4=t4: e.matmul(
                        bk[:, (hp % 4) * 128:(hp % 4 + 1) * 128], qpT[:, hp, t4 * 128:(t4 + 1) * 128],
                        keysT[:, hp % 2, :], start=True, stop=True),
                        reads=[T("qpT", hp), T("keysT")], writes=[T("pst", hp // 4)])
                for q4 in range(4):
                    mk.op("act", lambda e, q4=q4, sb_=sb_: e.activation(
                        out=sb_[:, q4 * 4:(q4 + 1) * 4, :], in_=mk.banks[q4][:].rearrange("p (a b) -> p a b", a=4),
                        func=AF.Copy), reads=[T("pst", q4)], writes=[T("scsb", tt % 2, q4)])
                for hp in range(16):
                    stok = T("scsb", tt % 2, hp // 4)
                    mk.op("dve", lambda e, hp=hp, sb_=sb_: e.max(out=vals[:, hp, 0:8], in_=sb_[:, hp, :]),
                          reads=[stok], writes=[T("vals", hp, 0)])
                    mk.op("dve", lambda e, hp=hp, sb_=sb_: e.max_index(out=idxu[:, hp, 0:8], in_max=vals[:, hp, 0:8],
                                                                     in_values=sb_[:, hp, :]),
                          reads=[stok, T("vals", hp, 0)], writes=[T("idxu", hp, 0)])
                    mk.op("dve", lambda e, hp=hp, sb_=sb_: e.match_replace(
                        out=work[:, 0:128], in_to_replace=vals[:, hp, 0:8], in_values=sb_[:, hp, :], imm_value=-1e30),
                        reads=[stok, T("vals", hp, 0)], writes=[T("work")])
                    mk.op("dve", lambda e, hp=hp: e.max(out=vals[:, hp, 8:16], in_=work[:, 0:128]),
                          reads=[T("work")], writes=[T("vals", hp, 1)])
                    mk.op("dve", lambda e, hp=hp: e.max_index(out=idxu[:, hp, 8:16], in_max=vals[:, hp, 8:16],
                                                            in_values=work[:, 0:128]),
                          reads=[T("work"), T("vals", hp, 1)], writes=[T("idxu", hp, 1)])
                allv = [T("vals", hp, k) for hp in range(16) for k in range(2)]
                alli = [T("idxu", hp, k) for hp in range(16) for k in range(2)]
                mk.op("dve", lambda e: e.tensor_copy(out=idf, in_=idxu), reads=alli, writes=[T("idf")])
                mk.op("dve", lambda e: e.tensor_scalar(out=idfa, in0=idf, scalar1=128.0, scalar2=None, op0=ALU.mult),
                      reads=[T("idf")], writes=[T("idfa")])
                for h in range(8):
                    mk.op("dve", lambda e, h=h: e.tensor_tensor(
                        out=cand[:, h, :].rearrange("p (a b) -> p a b", a=16),
                        in0=vals[:, 2 * h, :].unsqueeze(2).to_broadcast([128, 16, 16]),
                        in1=vals[:, 2 * h + 1, :].unsqueeze(1).to_broadcast([128, 16, 16]), op=ALU.add),
                        reads=allv, writes=[T("cand", h)])
                    mk.op("dve", lambda e, h=h: e.tensor_tensor(
                        out=cidx[:, h, :].rearrange("p (a b) -> p a b", a=16),
                        in0=idfa[:, 2 * h, :].unsqueeze(2).to_broadcast([128, 16, 16]),
                        in1=idf[:, 2 * h + 1, :].unsqueeze(1).to_broadcast([128, 16, 16]), op=ALU.add),
                        reads=[T("idf"), T("idfa")], writes=[T("cidx", h)])
                allc = [T("cand", h) for h in range(8)]
                candu = cand.bitcast(U32)
                mk.op("dve", lambda e: e.tensor_tensor(out=candu, in0=candu,
                                                       in1=cmask.unsqueeze(1).to_broadcast([128, 8, 256]),
                                                       op=ALU.bitwise_and), reads=allc + [T("cmask")], writes=allc)
                mk.op("dve", lambda e: e.tensor_tensor(out=candu, in0=candu,
                                                       in1=cpos.unsqueeze(1).to_broadcast([128, 8, 256]),
                                                       op=ALU.bitwise_or), reads=allc + [T("cpos")], writes=allc)
                alle = [T("eidf", k) for k in range(128)]
                mk.op("dve", lambda e: e.memset(eidf, 0.0), writes=alle)
                for h in range(8):
                    mk.op("dve", lambda e, h=h: e.max(out=tv[:, h, 0:8], in_=cand[:, h, :]),
                          reads=[T("cand", h)], writes=[T("tv", h, 0)])
                    mk.op("dve", lambda e, h=h: e.match_replace(out=work, in_to_replace=tv[:, h, 0:8],
                                                              in_values=cand[:, h, :], imm_value=-1e30),
                          reads=[T("cand", h), T("tv", h, 0)], writes=[T("work")])
                    mk.op("dve", lambda e, h=h: e.max(out=tv[:, h, 8:16], in_=work),
                          reads=[T("work")], writes=[T("tv", h, 1)])
                    for k in range(16):
                        mk.op("dve", lambda e, h=h, k=k: e.scalar_tensor_tensor(
                            out=junk, in0=cand[:, h, :], scalar=tv[:, h, k:k + 1], in1=cidx[:, h, :],
                            op0=ALU.is_equal, op1=ALU.mult, accum_out=eidf[:, h * 16 + k:h * 16 + k + 1]),
                            reads=[T("cand", h), T("cidx", h), T("tv", h, k // 8), T("eidf", h * 16 + k)],
                            writes=[T("eidf", h * 16 + k)])
                alltv = [T("tv", h, k) for h in range(8) for k in range(2)]
                mk.op("dve", lambda e: e.tensor_tensor(out=ex, in0=tv, in1=tv[:, :, 0:1].to_broadcast([128, 8, 16]),
                                                       op=ALU.subtract), reads=alltv, writes=[T("ex")])
                mk.op("act", lambda e: e.activation(out=ex, in_=ex, func=AF.Exp), reads=[T("ex")], writes=[T("ex")])
                mk.op("dve", lambda e: e.reduce_sum(out=ssum, in_=ex, axis=AX.X), reads=[T("ex")], writes=[T("ssum")])
                mk.op("dve", lambda e: e.reciprocal(out=ssum, in_=ssum), reads=[T("ssum")], writes=[T("ssum")])
                mk.op("dve", lambda e, tt=tt: e.tensor_tensor(
                    out=gate[:, tt, :].rearrange("p (h k) -> p h k", h=8), in0=ex,
                    in1=ssum.unsqueeze(2).to_broadcast([128, 8, 16]), op=ALU.mult),
                    reads=[T("ex"), T("ssum")], writes=[T("gate", tt)])
                mk.op("dve", lambda e, tt=tt: e.tensor_copy(out=eid[:, tt, :], in_=eidf),
                      reads=alle, writes=[T("eid", tt)])
    if "dbg_eid" in d:
        mk.dma("sp", d["dbg_eid"], eid, reads=[T("eid", tt) for tt in range(8)], final=True)
        mk.dma("sp", d["dbg_gate"], gate, reads=[T("gate", tt) for tt in range(8)], final=True)
    ytok = mk.alloc([128, 8, 2048], F32)
    with mk.scope():
        NB = 3
        ub = [mk.alloc([128, 2048], F32) for _ in range(NB)]
        vb = [mk.alloc([128, 2048], F32) for _ in range(NB)]
        junk2 = mk.alloc([128, 2048], BF16)
        dots = [mk.alloc([128, 128], F32) for _ in range(2)]
        coef = [mk.alloc([128, 128], F32) for _ in range(2)]
        items = []
        for step in range(9):
            for s in range(128):
                if step < 8:
                    items.append(("u", step, s))
                if step >= 1:
                    items.append(("v", step - 1, s))
            if step < 8:
                items.append(("g", step, 0))
        slot_of = {}
        order = {"u": [i for i, it_ in enumerate(items) if it_[0] == "u"],
                 "v": [i for i, it_ in enumerate(items) if it_[0] == "v"]}
        ordinal = {}
        for kind_ in ("u", "v"):
            for k_, i_ in enumerate(order[kind_]):
                ordinal[i_] = k_
        issued = {"u": 0, "v": 0}

        def issue_upto(kind, n):
            bufs = ub if kind == "u" else vb
            src = d["peer_u"] if kind == "u" else d["peer_v"]
            while issued[kind] < min(n, len(order[kind])):
                k_ = issued[kind]
                idx_ = order[kind][k_]
                _, t_, s = items[idx_]
                b = k_ % NB
                slot_of[idx_] = b
                issued[kind] += 1
                mk.op("pool", lambda e, b=b, t_=t_, s=s, bufs=bufs, src=src: e.indirect_dma_start(
                    out=bufs[b], out_offset=None, in_=src,
                    in_offset=bass.IndirectOffsetOnAxis(ap=eid[:, t_, s:s + 1], axis=0)),
                    reads=[T("eid", t_)], writes=[T(kind + "b", b)], dma=True)

        for idx, (kind, t_, s) in enumerate(items):
            if kind in ("u", "v"):
                issue_upto(kind, ordinal[idx] + NB)
                other = "v" if kind == "u" else "u"
                nxt = [i_ for i_ in order[other][issued[other]:issued[other] + 1]]
                if nxt and nxt[0] < idx + 2 * NB:
                    k_o = issued[other]
                    if k_o < NB or order[other][k_o - NB] < idx:
                        issue_upto(other, k_o + 1)
            if kind == "u":
                b = slot_of[idx]
                if s == 0:
                    mk.op("dve", lambda e, t_=t_: e.memset(dots[t_ % 2], 0.0),
                          writes=[T("dots", t_ % 2, s_) for s_ in range(128)])
                mk.op("dve", lambda e, b=b, t_=t_, s=s: e.scalar_tensor_tensor(
                    out=junk2, in0=ub[b], scalar=1.0, in1=h2tok[:, t_, :], op0=ALU.mult, op1=ALU.mult,
                    accum_out=dots[t_ % 2][:, s:s + 1]),
                    reads=[T("ub", b), T("h2tok", t_ // 4), T("dots", t_ % 2, s)], writes=[T("dots", t_ % 2, s)])
            elif kind == "v":
                b = slot_of[idx]
                if s == 0:
                    mk.op("dve", lambda e, b=b, t_=t_: e.tensor_scalar(
                        out=ytok[:, t_, :], in0=vb[b], scalar1=coef[t_ % 2][:, 0:1], scalar2=None, op0=ALU.mult),
                        reads=[T("vb", b), T("coef", t_ % 2)], writes=[("ytok", t_)])
                else:
                    mk.op("dve", lambda e, b=b, t_=t_, s=s: e.scalar_tensor_tensor(
                        out=ytok[:, t_, :], in0=vb[b], scalar=coef[t_ % 2][:, s:s + 1], in1=ytok[:, t_, :],
                        op0=ALU.mult, op1=ALU.add),
                        reads=[T("vb", b), T("coef", t_ % 2), ("ytok", t_)], writes=[("ytok", t_)])
            else:
                j = t_ % 2
                mk.op("act", lambda e, j=j: e.activation(out=coef[j], in_=dots[j], func=AF.Gelu),
                      reads=[T("dots", j, s_) for s_ in range(128)], writes=[T("coef", j)])
                mk.op("dve", lambda e, j=j, t_=t_: e.tensor_tensor(out=coef[j], in0=coef[j], in1=gate[:, t_, :], op=ALU.mult),
                      reads=[T("coef", j), T("gate", t_)], writes=[T("coef", j)])
    resid_ln_phase(mk, d, pfx + "L", ytok=ytok, zT=h2tok.rearrange("p a (b c) -> p (a b) c", b=2), xin="x1T",
                   xout="x2T", gidx=5, lng_k="ln2g", lnb_k="ln2b")


def build_mod():
    nc = bass.Bass("TRN2", target_bir_lowering=False)
    cT = nc.dram_tensor("cT", [128, 16, 2], F32, kind="ExternalInput").ap()
    wm = nc.dram_tensor("wm", [2, 2048, 1536], F32, kind="ExternalInput").ap()
    bm = nc.dram_tensor("bm", [2, 2, 1536], F32, kind="ExternalInput").ap()
    mo = nc.dram_tensor("mo", [2, 2, 1536], F32, kind="ExternalOutput").ap()
    mk = MK(nc)
    cT_sb = mk.sbuf([128, 16, 2], F32)
    wbuf = [mk.sbuf([128, 16, 768], F32) for _ in range(2)]
    bm_sb = mk.sbuf([2, 2, 1536], F32)
    res = mk.sbuf([2, 2, 1536], F32)
    ps = [mk.psum([128, 512], F32) for _ in range(2)]
    mk.dma("sp", cT_sb[:], cT, writes=["cT"])
    for l in range(2):
        mk.dma("sp", bm_sb[:, l, :], bm[l], writes=[("bm", l)])
    it = 0
    pi = 0
    for l in range(2):
        wv = wm[l].rearrange("(kc p) n -> p kc n", p=128)
        for h in range(2):
            wb = wbuf[it % 2]
            tok = ("w", it % 2)
            for q in range(4):
                mk.dma("sp" if q % 2 == 0 else "act", wb[:, q * 4:(q + 1) * 4, :],
                       wv[:, q * 4:(q + 1) * 4, h * 768:(h + 1) * 768], writes=[(tok, q)])
            for cb in range(2):
                n0 = cb * 512
                n1 = min(768, n0 + 512)
                p = ps[pi % 2]
                ptok = ("ps", pi % 2)
                pi += 1
                for kc in range(16):
                    mk.op("pe", lambda e, p=p, kc=kc, wb=wb, n0=n0, n1=n1: e.matmul(
                        p[0:2, 0:n1 - n0], cT_sb[:, kc, :], wb[:, kc, n0:n1],
                        start=(kc == 0), stop=(kc == 15)),
                        reads=["cT", (tok, kc // 4)], writes=[ptok])
                c0 = h * 768 + n0
                mk.op("dve", lambda e, p=p, l=l, c0=c0, n=n1 - n0: e.tensor_tensor(
                    out=res[:, l, c0:c0 + n], in0=p[0:2, 0:n], in1=bm_sb[:, l, c0:c0 + n], op=ALU.add),
                    reads=[ptok, ("bm", l)], writes=[("res", l, c0)])
            it += 1
        mk.dma("sp", mo[l], res[:, l, :], reads=[("res", l, c0) for c0 in (0, 512, 768, 1280)], final=True)
    mk.finalize()
    return nc


def build_a():
    nc = bass.Bass("TRN2", target_bir_lowering=False)
    d = {}

    def din(name, shape, dt=F32):
        d[name] = nc.dram_tensor(name, shape, dt, kind="ExternalInput").ap()

    def dout(name, shape, dt=BF16):
        d[name] = nc.dram_tensor(name, shape, dt, kind="ExternalOutput").ap()
    din("xT", [2048, 1024]); din("modT", [128, 6, 16]); din("w_in", [2048, 11520])
    din("qg", [128, 1]); din("kg", [128, 1]); din("cos", [128, 1024]); din("sin", [128, 1024])
    din("rmT", [128, 128], BF16); din("onesm", [128, 128], BF16)
    dout("qT", [22, 128, 1024]); dout("kT", [10, 128, 1024]); dout("v", [1024, 1280]); dout("gates", [48, 128, 1024])
    mk = MK(nc)
    mk.init_mem()
    stage_a(mk, d)
    mk.finalize()
    return nc


def build_b():
    nc = bass.Bass("TRN2", target_bir_lowering=False)
    d = {}

    def din(name, shape, dt=F32):
        d[name] = nc.dram_tensor(name, shape, dt, kind="ExternalInput").ap()
    din("qT", [22, 128, 1024], BF16); din("kTA", [2, 128, 4096], BF16); din("vA", [4096, 256], BF16)
    din("kTw", [8, 128, WIN * 128], BF16); din("vw", [WIN * 128, 1024], BF16); din("kvalid", [128, WIN])
    din("biasC", [128, 3, 8, 128]); din("biasB", [128, 25, 256]); din("sink", [128, 8]); din("ones", [128, 128], BF16)
    din("gates", [48, 128, 1024], BF16); din("w_pa", [1024, 2048]); din("w_pb", [768, 2048]); din("w_pc", [1024, 2048])
    din("w_o", [2048, 2048]); din("xT", [2048, 1024]); din("modT", [128, 6, 16]); din("ln1g", [128, 16]); din("ln1b", [128, 16])
    din("w_q", [2048, 2048]); din("keysT", [128, 2, 128])
    din("peer_u", [16384, 2048]); din("peer_v", [16384, 2048]); din("identf", [128, 128])
    din("ln2g", [128, 16]); din("ln2b", [128, 16]); din("cmask", [128, 256], U32); din("cpos", [128, 256], U32)
    d["x1T"] = nc.dram_tensor("x1T", [2048, 1024], F32, kind="Internal").ap()
    d["x2T"] = nc.dram_tensor("x2T", [2048, 1024], F32, kind="ExternalOutput").ap()
    mk = MK(nc)
    mk.init_mem()
    with mk.scope():
        oT = mk.alloc([128, 22, 1024], BF16)
        merged = mk.alloc([128, 16, 1024], BF16)
        attn_a(mk, d, oT)
        attn_c(mk, d, oT)
        attn_b(mk, d, oT)
        merge_phase(mk, d, oT, merged)
        wo_ln_phase(mk, d, merged)
    peer_phase(mk, d)
    mk.finalize()
    return nc


def _fm(vec):
    return np.ascontiguousarray(np.asarray(vec, np.float32).reshape(16, 128).T)


def kernel(x, c, w_mod, b_mod, w_in, a_q_gain, a_k_gain, c_sink, w_pa, w_pb, w_pc, w_o,
           ln1_g, ln1_b, peer_wq, peer_keys, peer_u, peer_v, ln2_g, ln2_b):
    f32 = lambda a: np.ascontiguousarray(np.asarray(a, np.float32))
    x = f32(x); c = f32(c)
    ncores = 8
    cores = list(range(ncores))
    cT = np.ascontiguousarray(c.T.reshape(16, 128, 2).transpose(1, 0, 2))
    w_mod = np.asarray(w_mod, np.float32); b_mod = np.asarray(b_mod, np.float32)
    in_maps = []
    for i in cores:
        sl = slice(i * 1536, (i + 1) * 1536)
        in_maps.append({"cT": cT, "wm": np.ascontiguousarray(w_mod[:, :, sl]),
                        "bm": np.ascontiguousarray(np.broadcast_to(b_mod[:, None, sl], (2, 2, 1536)))})
    res = run_bass_kernel_spmd(build_mod(), in_maps, core_ids=cores)
    mod = np.concatenate([np.asarray(r["mo"]) for r in res.results], axis=-1)

    bc, bb = bias_tables()
    rmT = rot_matrix_T()
    onesm = np.full((128, 128), 1.0 / 128, NPBF)
    ones = np.ones((128, 128), NPBF)
    identf = np.eye(128, dtype=np.float32)
    ropes = [rope_tables(np.arange(r * 1024, (r + 1) * 1024)) for r in range(4)]
    xT = [np.ascontiguousarray(x[i // 4, (i % 4) * 1024:(i % 4 + 1) * 1024, :].T) for i in cores]
    nc_a = build_a()
    nc_b = build_b()
    for l in range(2):
        modT = [np.ascontiguousarray(mod[l, b].reshape(6, 16, 128).transpose(2, 0, 1)) for b in range(2)]
        w_in_l = f32(w_in[l])
        in_maps = []
        for i in cores:
            cos, sin = ropes[i % 4]
            in_maps.append({"xT": xT[i], "modT": modT[i // 4], "w_in": w_in_l,
                            "qg": f32(a_q_gain[l]).reshape(128, 1), "kg": f32(a_k_gain[l]).reshape(128, 1),
                            "cos": cos, "sin": sin, "rmT": rmT, "onesm": onesm})
        ra = run_bass_kernel_spmd(nc_a, in_maps, core_ids=cores).results
        A = [{k: np.asarray(ra[i][k]) for k in ("qT", "kT", "v", "gates")} for i in cores]
        common = {"biasC": bc, "biasB": bb,
                  "sink": np.ascontiguousarray(np.broadcast_to(f32(c_sink[l]).reshape(1, 8), (128, 8))),
                  "ones": ones, "w_pa": f32(w_pa[l]), "w_pb": f32(w_pb[l]), "w_pc": f32(w_pc[l]), "w_o": f32(w_o[l]),
                  "ln1g": _fm(ln1_g[l]), "ln1b": _fm(ln1_b[l]), "w_q": f32(peer_wq[l]),
                  "keysT": np.ascontiguousarray(f32(peer_keys[l]).transpose(2, 0, 1)),
                  "peer_u": f32(peer_u[l]), "peer_v": f32(peer_v[l]), "identf": identf,
                  "ln2g": _fm(ln2_g[l]), "ln2b": _fm(ln2_b[l]),
                  "cmask": np.full((128, 256), 0xFFFFFF00, np.uint32),
                  "cpos": np.ascontiguousarray(np.broadcast_to(np.arange(256, dtype=np.uint32), (128, 256)))}
        in_maps = []
        for b in range(2):
            kv = gather_kv([A[b * 4 + r]["kT"] for r in range(4)], [A[b * 4 + r]["v"] for r in range(4)])
            for r in range(4):
                i = b * 4 + r
                m = dict(kv[r])
                m.update(common)
                m.update({"qT": A[i]["qT"], "gates": A[i]["gates"], "xT": xT[i], "modT": modT[b]})
                in_maps.append(m)
        rb = run_bass_kernel_spmd(nc_b, in_maps, core_ids=cores).results
        xT = [np.asarray(rb[i]["x2T"]) for i in cores]
    out = np.empty((2, 4096, 2048), np.float32)
    for i in cores:
        out[i // 4, (i % 4) * 1024:(i % 4 + 1) * 1024, :] = xT[i].T
    return out
```
